# Optimizing a Trainium2 kernel written in Bass

```python
import jax, jax.numpy as jnp
from jax import lax
import numpy as np

D_MODEL = 1024
BATCH = 4
SEQ = 4096
DEPTH = 4
DEC_BATCH = 32
DEC_SEQ = 8
PAST_LEN = 8192
PAGE_SIZE = 128

N_MIXERS = 2
N_LRU = (DEPTH + 1) // 2
N_ATT = DEPTH // 2
D_RNN = (3 * D_MODEL) // 2
N_BLOCKS = 16
BLOCK_W = D_RNN // N_BLOCKS
CONV_W = 4
LRU_C = 8.0
N_HEADS = 16
HEAD_DIM = D_MODEL // N_HEADS
D_ATT = N_HEADS * HEAD_DIM
WINDOWS = (128, 512, 2048)
DILATIONS = (1, 4, 16)
N_GROUPS = 3
QKV_W = 3 * N_GROUPS * D_ATT
ATT_SCALE = HEAD_DIM ** -0.5
NEG_INF = -1e30
EPS = 1e-6

kernel_name = "hybrid_rglru_dilated_swa_adaln_step"


def rmsnorm(x, g):
    xf = x.astype(jnp.float32)
    y = xf * lax.rsqrt(jnp.mean(xf * xf, axis=-1, keepdims=True) + EPS)
    return (y * g.astype(jnp.float32)).astype(x.dtype)


def adaln(x, c, g, w, b):
    mod = jax.nn.silu(c) @ w + b
    shift, scale, gate = jnp.split(mod[:, None, :], 3, axis=-1)
    return rmsnorm(x, g) * (1 + scale) + shift, gate


def alibi_slopes():
    return 2.0 ** (-8.0 * jnp.arange(1, N_HEADS + 1, dtype=jnp.float32) / N_HEADS)


def lru_branch(h, conv_buf, h0, w_in, conv_w, conv_b, wa, ba, wx, bx, lam, w_out):
    B, T, _ = h.shape
    xb, gb = jnp.split(h @ w_in, 2, axis=-1)
    xpad = jnp.concatenate([conv_buf.astype(xb.dtype), xb], axis=1)
    xc = conv_b + sum(xpad[:, k:k + T] * conv_w[k] for k in range(CONV_W))
    new_buf = xpad[:, -(CONV_W - 1):]
    xblk = xc.reshape(B, T, N_BLOCKS, BLOCK_W)
    r = jax.nn.sigmoid(jnp.einsum('btni,nij->btnj', xblk, wa).reshape(B, T, D_RNN).astype(jnp.float32) + ba.astype(jnp.float32))
    ig = jax.nn.sigmoid(jnp.einsum('btni,nij->btnj', xblk, wx).reshape(B, T, D_RNN).astype(jnp.float32) + bx.astype(jnp.float32))
    log_a = -LRU_C * r * jax.nn.softplus(-lam.astype(jnp.float32))
    a = jnp.exp(log_a)
    u = jnp.sqrt(-jnp.expm1(2.0 * log_a)) * ig * xc.astype(jnp.float32)

    def step(hc, au):
        hc = au[0] * hc + au[1]
        return hc, hc

    hT, hs = lax.scan(step, h0.astype(jnp.float32), (a.transpose(1, 0, 2), u.transpose(1, 0, 2)))
    y = hs.transpose(1, 0, 2).astype(h.dtype) * jax.nn.silu(gb)
    return y @ w_out, new_buf, hT.astype(h0.dtype)


def dilated_attn_prompt(q, k, v, window, dil, slopes):
    B, S, H, hd = q.shape
    n = window // dil
    blk = n
    L = S // dil
    n_chunks = -(-L // blk)
    Lp = n_chunks * blk

    def to_res(t, front, back):
        t = t.reshape(B, L, dil, H, hd).transpose(0, 2, 1, 3, 4)
        return jnp.pad(t, ((0, 0), (0, 0), (front, back), (0, 0), (0, 0)))

    qr = to_res(q, 0, Lp - L).reshape(B, dil, n_chunks, blk, H, hd)
    kr = to_res(k, blk, Lp - L)
    vr = to_res(v, blk, Lp - L)

    def pairs(t):
        return jnp.concatenate([t[:, :, :Lp].reshape(B, dil, n_chunks, blk, H, hd),
                                t[:, :, blk:].reshape(B, dil, n_chunks, blk, H, hd)], axis=3)

    kp, vp = pairs(kr), pairs(vr)
    ii = jnp.arange(blk)[:, None]
    jj = jnp.arange(2 * blk)[None, :]
    diff = ii - jj + blk
    penalty = slopes[:, None, None] * (diff * dil).astype(jnp.float32)
    band = (diff >= 0) & (diff <= n)

    def one_chunk(args):
        qb, kb, vb, c = args
        s = jnp.einsum('brihd,brjhd->brhij', qb, kb).astype(jnp.float32) * ATT_SCALE - penalty
        valid = band & (c * blk - blk + jj >= 0)
        s = jnp.where(valid, s, NEG_INF)
        lse = jax.nn.logsumexp(s, axis=-1)
        p = jnp.exp(s - lse[..., None]).astype(vb.dtype)
        return jnp.einsum('brhij,brjhd->brihd', p, vb), lse

    mv = lambda t: jnp.moveaxis(t, 2, 0)
    o, lse = lax.map(one_chunk, (mv(qr), mv(kp), mv(vp), jnp.arange(n_chunks)))
    o = jnp.moveaxis(o, 0, 2).reshape(B, dil, Lp, H, hd)[:, :, :L]
    o = o.transpose(0, 2, 1, 3, 4).reshape(B, S, H, hd)
    lse = jnp.swapaxes(jnp.moveaxis(lse, 0, 2), -1, -2).reshape(B, dil, Lp, H)[:, :, :L]
    lse = lse.transpose(0, 2, 1, 3).reshape(B, S, H)
    return o, lse


def dilated_attn_sample(q, k_all, v_all, buf_len, window, dil, slopes):
    B, T, H, hd = q.shape
    n = window // dil
    steps = jnp.arange(n + 1)
    idx = buf_len + jnp.arange(T)[:, None] - dil * steps[None, :]
    valid = idx >= 0
    idx = jnp.maximum(idx, 0)
    kg = k_all[:, idx]
    vg = v_all[:, idx]
    s = jnp.einsum('bthd,btkhd->bhtk', q, kg).astype(jnp.float32) * ATT_SCALE \
        - slopes[:, None, None] * (dil * steps).astype(jnp.float32)
    s = jnp.where(valid, s, NEG_INF)
    lse = jax.nn.logsumexp(s, axis=-1)
    p = jnp.exp(s - lse[..., None]).astype(vg.dtype)
    o = jnp.einsum('bhtk,btkhd->bthd', p, vg)
    return o, lse.transpose(0, 2, 1)


def att_branch(h, w_in, w_out, bufs, slopes):
    B, T, _ = h.shape
    proj = h @ w_in
    qkv = proj[..., :QKV_W].reshape(B, T, N_GROUPS, 3, N_HEADS, HEAD_DIM)
    gate = proj[..., QKV_W:]
    outs, lses, new_kv = [], [], []
    for g in range(N_GROUPS):
        q, k, v = qkv[:, :, g, 0], qkv[:, :, g, 1], qkv[:, :, g, 2]
        if bufs is None:
            o, lse = dilated_attn_prompt(q, k, v, WINDOWS[g], DILATIONS[g], slopes)
            new_kv.append(jnp.stack([k, v], axis=2)[:, T - min(WINDOWS[g], T):])
        else:
            buf = bufs[g].astype(k.dtype)
            k_all = jnp.concatenate([buf[:, :, 0], k], axis=1)
            v_all = jnp.concatenate([buf[:, :, 1], v], axis=1)
            o, lse = dilated_attn_sample(q, k_all, v_all, buf.shape[1], WINDOWS[g], DILATIONS[g], slopes)
            new_kv.append(jnp.stack([k, v], axis=2))
        outs.append(o)
        lses.append(lse)
    wgt = jax.nn.softmax(jnp.stack(lses, axis=0), axis=0)
    o = jnp.einsum('gbth,gbthd->bthd', wgt, jnp.stack(outs, axis=0).astype(jnp.float32))
    y = o.reshape(B, T, D_ATT).astype(h.dtype) * jax.nn.silu(gate)
    return y @ w_out, new_kv


def setup_inputs(seed: int = 0) -> dict:
    key = jax.random.key(seed)
    ks = jax.random.split(key, 32)
    nrm = lambda k, shape, s: jax.random.normal(k, shape, jnp.float32) * s
    buf_lens = [min(w, PAST_LEN) for w in WINDOWS]
    lam = jax.random.uniform(ks[20], (N_LRU, D_RNN), jnp.float32, 0.9, 0.999)
    return {
        "x_prompt": nrm(ks[0], (BATCH, SEQ, D_MODEL), 1.0),
        "x_sample": nrm(ks[1], (DEC_BATCH, DEC_SEQ, D_MODEL), 1.0),
        "state_conv": nrm(ks[2], (N_LRU, DEC_BATCH, CONV_W - 1, D_RNN), 0.5),
        "state_h": nrm(ks[3], (N_LRU, DEC_BATCH, D_RNN), 0.5),
        "cache_kv_g0": nrm(ks[4], (N_ATT, DEC_BATCH, buf_lens[0], 2, N_HEADS, HEAD_DIM), 1.0),
        "cache_kv_g1": nrm(ks[5], (N_ATT, DEC_BATCH, buf_lens[1], 2, N_HEADS, HEAD_DIM), 1.0),
        "cache_kv_g2": nrm(ks[6], (N_ATT, DEC_BATCH, buf_lens[2], 2, N_HEADS, HEAD_DIM), 1.0),
        "c_prompt": nrm(ks[7], (BATCH, D_MODEL), 1.0),
        "c_sample": nrm(ks[8], (DEC_BATCH, D_MODEL), 1.0),
        "norm_g": 1.0 + nrm(ks[9], (DEPTH, D_MODEL), 0.02),
        "ada_w": nrm(ks[10], (DEPTH, D_MODEL, 3 * D_MODEL), 0.3 * D_MODEL ** -0.5),
        "ada_b": nrm(ks[11], (DEPTH, 3 * D_MODEL), 0.02),
        "final_g": 1.0 + nrm(ks[12], (D_MODEL,), 0.02),
        "lru_w_in": nrm(ks[13], (N_LRU, D_MODEL, 2 * D_RNN), D_MODEL ** -0.5),
        "lru_conv_w": nrm(ks[14], (N_LRU, CONV_W, D_RNN), CONV_W ** -0.5),
        "lru_conv_b": nrm(ks[15], (N_LRU, D_RNN), 0.02),
        "lru_wa": nrm(ks[16], (N_LRU, N_BLOCKS, BLOCK_W, BLOCK_W), BLOCK_W ** -0.5),
        "lru_ba": nrm(ks[17], (N_LRU, D_RNN), 0.02),
        "lru_wx": nrm(ks[18], (N_LRU, N_BLOCKS, BLOCK_W, BLOCK_W), BLOCK_W ** -0.5),
        "lru_bx": nrm(ks[19], (N_LRU, D_RNN), 0.02),
        "lru_lambda": jnp.log(lam) - jnp.log1p(-lam),
        "lru_w_out": nrm(ks[21], (N_LRU, D_RNN, D_MODEL), D_RNN ** -0.5),
        "att_w_in": nrm(ks[22], (N_ATT, D_MODEL, QKV_W + D_ATT), D_MODEL ** -0.5),
        "att_w_out": nrm(ks[23], (N_ATT, D_ATT, D_MODEL), D_ATT ** -0.5),
    }


def reference(x_prompt, x_sample, state_conv, state_h, cache_kv_g0, cache_kv_g1, cache_kv_g2,
              c_prompt, c_sample, norm_g, ada_w, ada_b, final_g,
              lru_w_in, lru_conv_w, lru_conv_b, lru_wa, lru_ba, lru_wx, lru_bx, lru_lambda, lru_w_out,
              att_w_in, att_w_out):
    slopes = alibi_slopes()
    yp, ys = x_prompt, x_sample
    Bp = x_prompt.shape[0]
    p_conv, p_h, s_conv, s_h = [], [], [], []
    p_kv = [[] for _ in range(N_GROUPS)]
    s_kv = [[] for _ in range(N_GROUPS)]
    for i in range(DEPTH):
        hp, gp = adaln(yp, c_prompt, norm_g[i], ada_w[i], ada_b[i])
        hs, gs = adaln(ys, c_sample, norm_g[i], ada_w[i], ada_b[i])
        j = i // N_MIXERS
        if i % N_MIXERS == 0:
            prm = (lru_w_in[j], lru_conv_w[j], lru_conv_b[j], lru_wa[j], lru_ba[j],
                   lru_wx[j], lru_bx[j], lru_lambda[j], lru_w_out[j])
            zbuf = jnp.zeros((Bp, CONV_W - 1, D_RNN), x_prompt.dtype)
            zh = jnp.zeros((Bp, D_RNN), x_prompt.dtype)
            op, bp, hTp = lru_branch(hp, zbuf, zh, *prm)
            os_, bs, hTs = lru_branch(hs, state_conv[j], state_h[j], *prm)
            p_conv.append(bp); p_h.append(hTp); s_conv.append(bs); s_h.append(hTs)
        else:
            op, kvp = att_branch(hp, att_w_in[j], att_w_out[j], None, slopes)
            os_, kvs = att_branch(hs, att_w_in[j], att_w_out[j],
                                  (cache_kv_g0[j], cache_kv_g1[j], cache_kv_g2[j]), slopes)
            for g in range(N_GROUPS):
                p_kv[g].append(kvp[g]); s_kv[g].append(kvs[g])
        yp = yp + gp * op
        ys = ys + gs * os_
    y_prompt = rmsnorm(yp, final_g)
    y_sample = rmsnorm(ys, final_g)
    return (y_prompt, y_sample,
            jnp.stack(p_conv), jnp.stack(p_h),
            jnp.stack(p_kv[0]), jnp.stack(p_kv[1]), jnp.stack(p_kv[2]),
            jnp.stack(s_conv), jnp.stack(s_h),
            jnp.stack(s_kv[0]), jnp.stack(s_kv[1]), jnp.stack(s_kv[2]))
```

```python
import concourse.bass as bass
import concourse.mybir as mybir

F32 = mybir.dt.float32
BF16 = mybir.dt.bfloat16
ALU = mybir.AluOpType
AF = mybir.ActivationFunctionType
AX = mybir.AxisListType

SEM_LIMIT = 2000


class Res:
    __slots__ = ("name", "w", "rd")

    def __init__(self, name):
        self.name = name
        self.w = None
        self.rd = []


class Eng:
    def __init__(self, fw, name, h, is_pe=False):
        self.fw = fw
        self.name = name
        self.h = h
        self.is_pe = is_pe
        self.sem = fw.new_sem(name)
        self.cnt = 0
        self.seen = {}
        self.n_inst = 0

    def _rotate(self):
        if self.cnt >= SEM_LIMIT:
            self.sem = self.fw.new_sem(self.name)
            self.cnt = 0


class FW:
    def __init__(self, nc, stack, same_engine_sync=True):
        self.nc = nc
        self.stack = stack
        self.nsem = 0
        self.same_engine_sync = same_engine_sync
        self.pe = Eng(self, "pe", nc.tensor, is_pe=True)
        self.act = Eng(self, "act", nc.scalar)
        self.dve = Eng(self, "dve", nc.vector)
        self.pool = Eng(self, "pool", nc.gpsimd)
        self.sp = Eng(self, "sp", nc.sync)
        self.engs = [self.pe, self.act, self.dve, self.pool, self.sp]
        self.dma_sems = {}
        for q in (self.sp, self.pool, self.act):
            self.dma_sems[q.name] = [[self.new_sem("d" + q.name), 0] for _ in range(8)]
        self.dma_rr = {q.name: 0 for q in (self.sp, self.pool, self.act)}
        self.out_tokens = []

    def new_sem(self, name):
        self.nsem += 1
        return self.stack.enter_context(self.nc.semaphore(f"{name}_{self.nsem}"))

    def _wait(self, eng, tok):
        if tok is None:
            return
        sem, val, src = tok
        if src is eng and (eng.is_pe or not self.same_engine_sync):
            return
        key = id(sem)
        if eng.seen.get(key, 0) >= val:
            return
        eng.h.wait_ge(sem, val)
        eng.seen[key] = val

    def _deps(self, eng, reads, writes):
        for r in reads:
            self._wait(eng, r.w)
        for w in writes:
            self._wait(eng, w.w)
            for t in w.rd:
                self._wait(eng, t)

    def _record(self, tok, reads, writes):
        for r in reads:
            r.rd.append(tok)
            if len(r.rd) > 64:
                r.rd = r.rd[-48:]
        for w in writes:
            w.w = tok
            w.rd = []

    def op(self, eng, fn, reads=(), writes=()):
        eng._rotate()
        self._deps(eng, reads, writes)
        ins = fn(eng.h)
        eng.cnt += 1
        eng.n_inst += 1
        ins.then_inc(eng.sem, 1)
        tok = (eng.sem, eng.cnt, eng)
        eng.last_tok = tok
        self._record(tok, reads, writes)
        return tok

    def dma(self, q, out, in_, reads=(), writes=(), is_output=False, **kw):
        pool = self.dma_sems[q.name]
        i = self.dma_rr[q.name]
        self.dma_rr[q.name] = (i + 1) % len(pool)
        slot = pool[i]
        sem = slot[0]
        if slot[1] > 0:
            self._wait(q, (sem, slot[1], None))
        if slot[1] >= SEM_LIMIT:
            slot[0] = self.new_sem("d" + q.name); slot[1] = 0
            sem = slot[0]
        self._deps(q, reads, writes)
        ins = q.h.dma_start(out=out, in_=in_, **kw)
        slot[1] += 16
        ins.then_inc(sem, 16)
        q.n_inst += 1
        tok = (sem, slot[1], None)
        self._record(tok, reads, writes)
        if is_output:
            self.out_tokens.append(tok)
        return tok

    def finish(self):
        for qn, pool in self.dma_sems.items():
            for sem, val in pool:
                if val > 0:
                    self._wait(self.sp, (sem, val, None))
        for e in self.engs:
            if e is not self.sp and getattr(e, "last_tok", None) is not None:
                self._wait(self.sp, e.last_tok)


class Pool:
    def __init__(self, tiles):
        self.tiles = tiles
        self.i = 0

    def get(self):
        t = self.tiles[self.i]
        self.i = (self.i + 1) % len(self.tiles)
        return t


class Tile:
    def __init__(self, h, name):
        self.h = h
        self.res = Res(name)

    def __getitem__(self, k):
        return self.h[k]

import numpy as np
import ml_dtypes
from contextlib import ExitStack
from concourse.bass_utils import run_bass_kernel_spmd

DM = 1024; SEQ = 4096; TH = 2048; DR = 1536; NHD = 16; HD = 64
DILS = (1, 4, 16); WINS = (128, 512, 2048)
SLOPES = [2.0 ** (-8.0 * (h + 1) / 16) for h in range(16)]
EPS = 1e-6
BIGD = 1.0e6
NB = 4
NS = 32

WSPEC = {"adaw": (4, 24576), "lruin": (2, 24576), "lruout": (2, 12288), "wa": (2, 4608),
         "wx": (2, 4608), "attin": (2, 81920), "attout": (2, 8192)}


def build(do_sample=True, n_layers=4, halves=(0, 1)):
    nc = bass.Bass("TRN2", target_bir_lowering=False)
    st = ExitStack()
    fw = FW(nc, st)
    din = lambda n, s, d=F32: nc.dram_tensor(n, list(s), d, kind="ExternalInput").ap()
    dout = lambda n, s, d=F32: nc.dram_tensor(n, list(s), d, kind="ExternalOutput").ap()
    dint = lambda n, s, d=F32: nc.dram_tensor(n, list(s), d, kind="Internal").ap()
    sb = lambda n, s, d=F32: Tile(st.enter_context(nc.sbuf_tensor(n, list(s), d)), n)
    pst = lambda n, s, d=F32: Tile(st.enter_context(nc.psum_tensor(n, list(s), d)), n)

    def R(ts):
        out = []
        for t in ts:
            r = t.res if isinstance(t, Tile) else t
            if isinstance(r, (list, tuple)):
                out.extend(r)
            else:
                out.append(r)
        return out

    def V(fn, r=(), w=()): return fw.op(fw.dve, fn, R(r), R(w))
    def A(fn, r=(), w=()): return fw.op(fw.act, fn, R(r), R(w))
    def G(fn, r=(), w=()): return fw.op(fw.pool, fn, R(r), R(w))
    def P(fn, r=(), w=()): return fw.op(fw.pe, fn, R(r), R(w))
    def DMA(out, in_, r=(), w=(), q=None, **kw): return fw.dma(q or fw.sp, out, in_, R(r), R(w), **kw)

    xp = din("xp", [128, 8, SEQ]); cp = din("cp", [128, 8])
    xs = din("xs", [128, 8, NS]); cs = din("cs", [128, 8, NB])
    sconv_in = din("sconv_in", [2, 128, 12, NB, 3]); sh_in = din("sh_in", [2, 128, 12, NB])
    caches = [din(f"cache{g}", [2, NB, WINS[g], 2048]) for g in range(3)]
    normg_d = din("normg", [128, 4, 8]); adab_d = din("adab", [128, 4, 24]); finalg_d = din("finalg", [128, 8])
    convw_d = din("convw", [128, 2, 4, 12]); convb_d = din("convb", [128, 2, 12])
    ba_d = din("ba", [128, 2, 12]); bx_d = din("bx", [128, 2, 12]); lam_d = din("lam", [128, 2, 12])
    wf = {k: din("w_" + k, [n, 128, m]) for k, (n, m) in WSPEC.items()}
    wb = {k: dint("wb_" + k, [n, 128, m], BF16) for k, (n, m) in WSPEC.items()}
    identf_d = din("identf", [128, 128]); identb_d = din("identb", [128, 256], BF16)
    onesdiv_d = din("onesdiv", [128, 128]); onesb_d = din("onesb", [128, 2, 128], BF16)
    dmat_d = din("dmat", [128, 256])
    yp_o = dout("yp_o", [128, 8, SEQ]); ys_o = dout("ys_o", [128, 8, NS])
    pconv_o = dout("pconv_o", [2, 128, 12, 3]); ph_o = dout("ph_o", [2, 128, 12])
    pkv_o = [dout(f"pkv{g}_o", [2, WINS[g], 2, 1024]) for g in range(3)]
    sconv_o = dout("sconv_o", [2, 128, 12, NB, 3]); sh_o = dout("sh_o", [2, 128, 12, NB])
    skv_o = [dout(f"skv{g}_o", [2, NB, 8, 2, 1024]) for g in range(3)]
    carK = dint("carK", [2, 8, 3, 128, 16, 128], BF16)
    carV = dint("carV", [2, 8, 3, 128, 16, 128], BF16)
    carK_res = {}; carV_res = {}

    wres = {(k, i): Res(f"w{k}{i}") for k, (n, m) in WSPEC.items() for i in range(n)}
    order = [("adaw", 0), ("adaw", 1), ("adaw", 2), ("adaw", 3), ("lruin", 0), ("wa", 0), ("wx", 0), ("lruout", 0),
             ("attin", 0), ("attout", 0), ("lruin", 1), ("wa", 1), ("wx", 1), ("lruout", 1), ("attin", 1), ("attout", 1)]
    for (k, i) in order:
        m = WSPEC[k][1]
        for a in range(0, m, 8192):
            b = min(m, a + 8192)
            DMA(wb[k][i, :, a:b], wf[k][i, :, a:b], w=[wres[k, i]], q=fw.pool)

    def load(name, src, shape, dt=F32):
        t = sb(name, shape, dt)
        DMA(t[:], src, w=[t])
        return t
    identf = load("identf_s", identf_d, [128, 128]); identb = load("identb_s", identb_d, [128, 256], BF16)
    onesdiv = load("onesdiv_s", onesdiv_d, [128, 128]); onesb = load("onesb_s", onesb_d, [128, 2, 128], BF16)
    dmat = load("dmat_s", dmat_d, [128, 256])
    normg = load("normg_s", normg_d, [128, 4, 8]); adab = load("adab_s", adab_d, [128, 4, 24])
    finalg = load("finalg_s", finalg_d, [128, 8])
    convw = load("convw_s", convw_d, [128, 2, 4, 12]); convb = load("convb_s", convb_d, [128, 2, 12])
    ba = load("ba_s", ba_d, [128, 2, 12]); bx = load("bx_s", bx_d, [128, 2, 12]); lam = load("lam_s", lam_d, [128, 2, 12])
    cpt = load("cp_s", cp, [128, 8]); cst = load("cs_s", cs, [128, 8, NB])
    zero8 = sb("zero8", [128, 8]); G(lambda e: e.memset(zero8[:], 0.0), w=[zero8])

    cA = sb("cA", [128, 2, 12]); cA2 = sb("cA2", [128, 2, 12])
    A(lambda e: e.activation(cA[:], lam[:], AF.Exp, scale=-1.0), r=[lam], w=[cA])
    A(lambda e: e.activation(cA[:], cA[:], AF.Ln, bias=1.0), r=[cA], w=[cA])
    V(lambda e: e.tensor_scalar(cA2[:], cA[:], -16.0, None, ALU.mult), r=[cA], w=[cA2])
    V(lambda e: e.tensor_scalar(cA[:], cA[:], -8.0, None, ALU.mult), r=[cA], w=[cA])

    mm_pool = Pool([pst(f"mm{i}", [128, 512]) for i in range(3)])
    s_pool = Pool([pst(f"sps{i}", [128, 512]) for i in range(2)])
    pt_ps = pst("ptps", [128, 512], BF16)
    olm_pool = Pool([pst(f"olm{i}", [128, 2, 128]) for i in range(2)])
    mb_ps = None

    wk_f = Pool([sb(f"wkf{i}", [128, 512]) for i in range(6)])
    rstd_t = sb("rstd_t", [128, 512])

    modp = sb("modp", [128, 4, 24]); mods = sb("mods", [128, 4, 24, NB])
    scp = sb("scp", [128, 8], BF16); scs = sb("scs", [128, 8, NB], BF16)
    A(lambda e: e.activation(scp[:], cpt[:], AF.Silu), r=[cpt], w=[scp])
    A(lambda e: e.activation(scs[:], cst[:], AF.Silu), r=[cst], w=[scs])
    awf = sb("awf", [128, 4096], BF16)
    class VW(Tile):
        def __init__(self, ap, res):
            self.h = ap; self.res = res
    adaw_t = VW(awf[:, 0:3072].rearrange("p (k n) -> p k n", k=8), awf.res)
    class _OnePool:
        def get(self): return adaw_t
    adaw_pool = _OnePool()
    for i in range(n_layers):
        wv = wb["adaw"][i].rearrange("p (k n) -> p k n", k=8)
        mp = mm_pool.get(); msp = mm_pool.get()
        for q in range(8):
            wt = adaw_pool.get()
            DMA(wt[:], wv[:, :, q * 384:(q + 1) * 384], r=[wres["adaw", i]], w=[wt])
            for o6 in range(3):
                o = q * 3 + o6
                for k in range(8):
                    P(lambda e: e.matmul(mp[:, o:o + 1], lhsT=wt[:, k, o6 * 128:(o6 + 1) * 128], rhs=scp[:, k:k + 1],
                                         start=(k == 0), stop=(k == 7)), r=[wt, scp], w=[mp])
                for k in range(8):
                    P(lambda e: e.matmul(msp[:, o * NB:(o + 1) * NB], lhsT=wt[:, k, o6 * 128:(o6 + 1) * 128], rhs=scs[:, k, :],
                                         start=(k == 0), stop=(k == 7)), r=[wt, scs], w=[msp])
        V(lambda e: e.tensor_tensor(modp[:, i, :], mp[:, 0:24], adab[:, i, :], ALU.add), r=[mp, adab], w=[modp])
        V(lambda e: e.tensor_tensor(mods[:, i, :, :], msp[:, 0:24 * NB].rearrange("p (o b) -> p o b", b=NB),
                                    adab[:, i, :].unsqueeze(2).to_broadcast([128, 24, NB]), ALU.add), r=[msp, adab], w=[mods])
    s1p = sb("s1p", [128, 4, 8])
    for i in range(n_layers):
        V(lambda e: e.scalar_tensor_tensor(s1p[:, i, :], modp[:, i, 8:16], 1.0, normg[:, i, :], ALU.add, ALU.mult), r=[modp, normg], w=[s1p])
    s1s = sb("s1s", [128, 4, 8, NB])
    for i in range(n_layers):
        V(lambda e: e.scalar_tensor_tensor(s1s[:, i, :, :], mods[:, i, 8:16, :], 1.0,
                                           normg[:, i, :].unsqueeze(2).to_broadcast([128, 8, NB]), ALU.add, ALU.mult), r=[mods, normg], w=[s1s])

    x = sb("x", [128, 8, TH]); hall = sb("hall", [128, 8, TH], BF16)
    hstate = sb("hstate", [128, 2, 12]); ctail = sb("ctail", [128, 2, 12, 3])

    def adaln_prompt(s1ap, s2ap, out_fn):
        for c in range(4):
            cs_ = slice(c * 512, (c + 1) * 512)
            ssq = mm_pool.get()
            for k in range(8):
                sq = wk_f.get()
                A(lambda e: e.activation(sq[:], x[:, k, cs_], AF.Square), r=[x], w=[sq])
                P(lambda e: e.matmul(ssq[:], lhsT=onesdiv[:], rhs=sq[:], start=(k == 0), stop=(k == 7)), r=[onesdiv, sq], w=[ssq])
            rstd = rstd_t
            A(lambda e: e.activation(rstd[:], ssq[:], AF.Sqrt, bias=EPS), r=[ssq], w=[rstd])
            V(lambda e: e.reciprocal(rstd[:], rstd[:]), r=[rstd], w=[rstd])
            for k in range(8):
                tmp = wk_f.get()
                V(lambda e: e.tensor_tensor(tmp[:], x[:, k, cs_], rstd[:], ALU.mult), r=[x, rstd], w=[tmp])
                out_fn(c, k, cs_, tmp)

    aw = VW(awf[:, 0:3072].rearrange("p (k n) -> p k n", k=8), awf.res)
    awg = VW(awf[:, 3072:4096].rearrange("p (k n) -> p k n", k=8), Res("awg"))
    awo = sb("awo", [128, 1024], BF16)
    qk = sb("qk", [128, 3, TH], BF16)
    qT = VW(qk[:, 0, :], Res("qT")); qT1 = VW(qk[:, 1, :], Res("qT1")); kT = VW(qk[:, 2, :], Res("kT"))
    qTm = [qT, qT1]
    vv = sb("vv", [128, 4, 16, 128], BF16)
    vt0 = VW(vv[:, 0], Res("vt0")); vt1 = VW(vv[:, 1], Res("vt1")); vprev0 = VW(vv[:, 2], Res("vp0")); vprev1 = VW(vv[:, 3], Res("vp1"))
    vtm = [vt0, vt1]; vprevm = [vprev0, vprev1]
    kprev = sb("kprev", [128, 16, 128], BF16)
    og = [sb(f"og{g}", [128, TH], BF16) for g in range(3)]
    lseg = [sb(f"lse{g}", [128, TH]) for g in range(3)]
    yhp = og[0]
    qkf = qk[:].rearrange("p a b -> p (a b)"); vvf = vv[:].rearrange("p a b c -> p (a b c)")
    lw_t = [VW(qkf[:, 2048:5120].rearrange("p (k n) -> p k n", k=8), [qT1.res, kT.res]),
            VW(vvf[:, 0:3072].rearrange("p (k n) -> p k n", k=8), [vt0.res, vt1.res])]
    class _LwPool:
        def __init__(self): self.i = 0
        def get(self):
            self.i ^= 1
            return lw_t[self.i ^ 1]
    lw_pool = _LwPool()
    lwo = VW(vvf[:, 3072:6144].rearrange("p (m n) -> p m n", m=3), [vt1.res, vprev0.res])
    lwx = VW(vvf[:, 6144:7296].rearrange("p (m n) -> p m n", m=3), [vprev1.res])
    lwa = VW(kprev[:, 0:9, :].rearrange("p a b -> p (a b)").rearrange("p (m n) -> p m n", m=3), kprev.res)
    xpad_t = [VW(lseg[m][:, 0:515], lseg[m].res) for m in range(3)]
    xc_t = [VW(lseg[m][:, 515:1027], lseg[m].res) for m in range(3)]
    xcb_t = [VW(qk[:, 0, m * 512:(m + 1) * 512], qT.res) for m in range(3)]
    ybuf = og

    def lru_prompt(i, j, half):
        s1k = lambda k: s1p[:, i, k:k + 1]; s2k = lambda k: modp[:, i, k:k + 1]
        def mk_h(c, k, cs_, tmp):
            A(lambda e: e.activation(hall[:, k, cs_], tmp[:], AF.Identity, bias=s2k(k), scale=s1k(k)), r=[tmp, s1p, modp], w=[hall])
        adaln_prompt(s1k, s2k, mk_h)
        if half == 0:
            G(lambda e: e.memset(hstate[:, j, :], 0.0), w=[hstate])
            G(lambda e: e.memset(ctail[:, j, :, :], 0.0), w=[ctail])
        win = wb["lruin"][j].rearrange("p (g t k n) -> p g t k n", g=4, t=2, k=8)
        wov = wb["lruout"][j].rearrange("p (g m n) -> p g m n", g=4, m=3)
        wav = wb["wa"][j].rearrange("p (g m n) -> p g m n", g=4, m=3)
        wxv = wb["wx"][j].rearrange("p (g m n) -> p g m n", g=4, m=3)
        for g4 in range(4):
            wxb = lw_pool.get(); wgb = lw_pool.get()
            DMA(wxb[:], win[:, g4, 0], r=[wres["lruin", j]], w=[wxb])
            DMA(wgb[:], win[:, g4, 1], r=[wres["lruin", j]], w=[wgb])
            DMA(lwo[:], wov[:, g4], r=[wres["lruout", j]], w=[lwo])
            DMA(lwa[:], wav[:, g4], r=[wres["wa", j]], w=[lwa])
            DMA(lwx[:], wxv[:, g4], r=[wres["wx", j]], w=[lwx])
            for c in range(4):
                cs_ = slice(c * 512, (c + 1) * 512)
                xcs = []; xcbs = []
                for m in range(3):
                    ct = g4 * 3 + m
                    pp = mm_pool.get()
                    for k in range(8):
                        P(lambda e: e.matmul(pp[:], lhsT=wxb[:, k, m * 128:(m + 1) * 128], rhs=hall[:, k, cs_], start=(k == 0), stop=(k == 7)), r=[wxb, hall], w=[pp])
                    xpad = xpad_t[m]
                    G(lambda e: e.tensor_copy(xpad[:, 0:3], ctail[:, j, ct, :]), r=[ctail], w=[xpad])
                    A(lambda e: e.copy(xpad[:, 3:515], pp[:]), r=[pp], w=[xpad])
                    G(lambda e: e.tensor_copy(ctail[:, j, ct, :], xpad[:, 512:515]), r=[xpad], w=[ctail])
                    xc = xc_t[m]
                    V(lambda e: e.tensor_scalar(xc[:], xpad[:, 0:512], convw[:, j, 0, ct:ct + 1], convb[:, j, ct:ct + 1], ALU.mult, ALU.add), r=[xpad, convw, convb], w=[xc])
                    for tp in range(1, 4):
                        V(lambda e: e.scalar_tensor_tensor(xc[:], xpad[:, tp:tp + 512], convw[:, j, tp, ct:ct + 1], xc[:], ALU.mult, ALU.add), r=[xpad, convw, xc], w=[xc])
                    xcb = xcb_t[m]
                    A(lambda e: e.copy(xcb[:], xc[:]), r=[xc], w=[xcb])
                    xcs.append(xc); xcbs.append(xcb)
                for m in range(3):
                    ct = g4 * 3 + m
                    kks = [kk for kk in range(3) if not ((m == 0 and kk == 2) or (m == 2 and kk == 0))]
                    rp = mm_pool.get()
                    for n_, kk in enumerate(kks):
                        P(lambda e: e.matmul(rp[:], lhsT=lwa[:, kk, m * 128:(m + 1) * 128], rhs=xcbs[kk][:], start=(n_ == 0), stop=(n_ == len(kks) - 1)), r=[lwa, xcbs[kk]], w=[rp])
                    rr = wk_f.get()
                    A(lambda e: e.activation(rr[:], rp[:], AF.Sigmoid, bias=ba[:, j, ct:ct + 1]), r=[rp, ba], w=[rr])
                    ip = mm_pool.get()
                    for n_, kk in enumerate(kks):
                        P(lambda e: e.matmul(ip[:], lhsT=lwx[:, kk, m * 128:(m + 1) * 128], rhs=xcbs[kk][:], start=(n_ == 0), stop=(n_ == len(kks) - 1)), r=[lwx, xcbs[kk]], w=[ip])
                    ig = wk_f.get()
                    A(lambda e: e.activation(ig[:], ip[:], AF.Sigmoid, bias=bx[:, j, ct:ct + 1]), r=[ip, bx], w=[ig])
                    aa = wk_f.get()
                    A(lambda e: e.activation(aa[:], rr[:], AF.Exp, scale=cA[:, j, ct:ct + 1]), r=[rr, cA], w=[aa])
                    A(lambda e: e.activation(rr[:], rr[:], AF.Exp, scale=cA2[:, j, ct:ct + 1]), r=[rr, cA2], w=[rr])
                    V(lambda e: e.tensor_scalar(rr[:], rr[:], -1.0, 1.0, ALU.mult, ALU.add), r=[rr], w=[rr])
                    V(lambda e: e.tensor_scalar(rr[:], rr[:], 0.0, None, ALU.max), r=[rr], w=[rr])
                    A(lambda e: e.activation(rr[:], rr[:], AF.Sqrt), r=[rr], w=[rr])
                    V(lambda e: e.tensor_tensor(ig[:], ig[:], rr[:], ALU.mult), r=[ig, rr], w=[ig])
                    V(lambda e: e.tensor_tensor(ig[:], ig[:], xcs[m][:], ALU.mult), r=[ig, xcs[m]], w=[ig])
                    hs = rr
                    V(lambda e: e.tensor_tensor_scan(hs[:], aa[:], ig[:], hstate[:, j, ct:ct + 1], ALU.mult, ALU.add), r=[aa, ig, hstate], w=[hs])
                    G(lambda e: e.tensor_copy(hstate[:, j, ct:ct + 1], hs[:, 511:512]), r=[hs], w=[hstate])
                    gp = mm_pool.get()
                    for k in range(8):
                        P(lambda e: e.matmul(gp[:], lhsT=wgb[:, k, m * 128:(m + 1) * 128], rhs=hall[:, k, cs_], start=(k == 0), stop=(k == 7)), r=[wgb, hall], w=[gp])
                    sg = aa
                    A(lambda e: e.activation(sg[:], gp[:], AF.Silu), r=[gp], w=[sg])
                    V(lambda e: e.tensor_tensor(ybuf[m][:, cs_], hs[:], sg[:], ALU.mult), r=[hs, sg], w=[ybuf[m]])
                for o in range(8):
                    op_ = mm_pool.get()
                    for m in range(3):
                        P(lambda e: e.matmul(op_[:], lhsT=lwo[:, m, o * 128:(o + 1) * 128], rhs=ybuf[m][:, cs_], start=(m == 0), stop=(m == 2)), r=[lwo, ybuf[m]], w=[op_])
                    V(lambda e: e.scalar_tensor_tensor(x[:, o, cs_], op_[:], modp[:, i, 16 + o:17 + o], x[:, o, cs_], ALU.mult, ALU.add), r=[op_, modp, x], w=[x])
        if half == 1:
            DMA(pconv_o[j], ctail[:, j, :, :], r=[ctail], is_output=True)
            DMA(ph_o[j], hstate[:, j, :], r=[hstate], is_output=True)

    ssb_pool = Pool([sb(f"ssb{i}", [128, 2, 256]) for i in range(1)])
    pb_pool = Pool([sb(f"pb{i}", [128, 2, 256], BF16) for i in range(1)])
    ptsb_pool = Pool([sb(f"ptsb{i}", [128, 2, 2, 128], BF16) for i in range(1)])
    m_pool = Pool([sb(f"mrow{i}", [128, 4]) for i in range(4)])
    mexp_pool = Pool([sb(f"mexp{i}", [128, 128]) for i in range(1)])
    kvst_pool = Pool([sb(f"kvst{i}", [128, 2, 128]) for i in range(2)])
    tmpc_pool = Pool([sb(f"tmpc{i}", [128, 128]) for i in range(2)])

    def attn_chunk(nq, q_ap_fn, ktiles, g, hp, o_dst_fn, lse_dst_fn, o_res, lse_res, qres, nw=None):
        nkt = len(ktiles); nk = nkt * 128; d0 = 256 - nk
        nw = nq if nw is None else nw
        sp_ = s_pool.get()
        for hh in range(2):
            for t, (kf, vf, kres, vres) in enumerate(ktiles):
                P(lambda e: e.matmul(sp_[:nq, hh * 256 + t * 128: hh * 256 + (t + 1) * 128], lhsT=q_ap_fn(hh), rhs=kf, start=True, stop=True), r=list(qres) + [kres], w=[sp_])
        ssb = ssb_pool.get()
        for hh in range(2):
            cgh = SLOPES[hp * 2 + hh] * DILS[g]
            V(lambda e: e.scalar_tensor_tensor(ssb[:nq, hh, 0:nk], dmat[:nq, d0:256], -cgh, sp_[:nq, hh * 256: hh * 256 + nk], ALU.mult, ALU.add), r=[dmat, sp_], w=[ssb])
        mr = m_pool.get()
        V(lambda e: e.tensor_reduce(mr[:nq, 0:2], ssb[:nq, :, 0:nk], AX.X, ALU.max), r=[ssb], w=[mr])
        V(lambda e: e.tensor_scalar(mr[:nq, 2:4], mr[:nq, 0:2], -1.0, None, ALU.mult), r=[mr], w=[mr])
        pb = pb_pool.get()
        for hh in range(2):
            A(lambda e: e.activation(pb[:nq, hh, 0:nk], ssb[:nq, hh, 0:nk], AF.Exp, bias=mr[:nq, 2 + hh:3 + hh]), r=[ssb, mr], w=[pb])
        for hh in range(2):
            for t in range(nkt):
                P(lambda e: e.transpose(pt_ps[:, (hh * 2 + t) * 128:(hh * 2 + t) * 128 + nq], pb[:, hh, t * 128:(t + 1) * 128], identb[:, :nq]), r=[pb, identb], w=[pt_ps])
        ptsb = ptsb_pool.get()
        A(lambda e: e.copy(ptsb[:, :, 0:nkt, 0:nq], pt_ps[:].rearrange("p (h t q) -> p h t q", h=2, t=2)[:, :, 0:nkt, 0:nq]), r=[pt_ps], w=[ptsb])
        mexp = mexp_pool.get()
        G(lambda e: e.tensor_copy(mexp[:nq, :].rearrange("p (h d) -> p h d", h=2), mr[:nq, 0:2].unsqueeze(2).to_broadcast([nq, 2, 64])), r=[mr], w=[mexp])
        olm = olm_pool.get()
        n_acc = 2 * nkt; n_ = 0
        for hh in range(2):
            for t, (kf, vf, kres, vres) in enumerate(ktiles):
                P(lambda e: e.matmul(olm[:, 0, 0:nq], lhsT=vf[hh], rhs=ptsb[:, hh, t, 0:nq], start=(n_ == 0), stop=(n_ == n_acc - 1)), r=list(vres) + [ptsb], w=[olm])
                n_ += 1
        n_ = 0
        for hh in range(2):
            for t in range(nkt):
                P(lambda e: e.matmul(olm[:, 1, 0:nq], lhsT=onesb[:, hh, :], rhs=ptsb[:, hh, t, 0:nq], start=(n_ == 0), stop=(n_ == n_acc - 1)), r=[onesb, ptsb], w=[olm])
                n_ += 1
        mb = sp_
        P(lambda e: e.matmul(mb[:, 0:nq], lhsT=mexp[:, :], rhs=identf[:, 0:nq], start=True, stop=True), r=[mexp, identf], w=[mb])
        rl = tmpc_pool.get(); ll = tmpc_pool.get()
        V(lambda e: e.reciprocal(rl[:, 0:nq], olm[:, 1, 0:nq]), r=[olm], w=[rl])
        V(lambda e: e.tensor_tensor(o_dst_fn(), olm[:, 0, 0:nw], rl[:, 0:nw], ALU.mult), r=[olm, rl], w=[o_res])
        A(lambda e: e.activation(ll[:, 0:nq], rl[:, 0:nq], AF.Ln), r=[rl], w=[ll])
        V(lambda e: e.tensor_tensor(lse_dst_fn(), mb[:, 0:nw], ll[:, 0:nw], ALU.subtract), r=[mb, ll], w=[lse_res])

    def combine(nt, ogs, lses, sg_fn, y_ap, yres, dils=DILS):
        mx = lses[0]
        t0 = wk_f;
        def nat(t, g, a, b):
            dil = dils[g]
            return t[:].rearrange("p (r s) -> p s r", r=dil)[:, a // dil:b // dil, :]
        def w3(t, g, n):
            return t[:, :n].rearrange("p (s r) -> p s r", r=dils[g])
        for a in range(0, nt, 512):
            b = min(nt, a + 512); n = b - a
            mxx = wk_f.get(); den = wk_f.get(); num = wk_f.get()
            G(lambda e: e.tensor_copy(w3(mxx, 1, n), nat(lses[1], 1, a, b)), r=[lses[1]], w=[mxx])
            V(lambda e: e.tensor_tensor(mxx[:, :n], mxx[:, :n], lses[0][:, a:b], ALU.max), r=[lses[0], mxx], w=[mxx])
            V(lambda e: e.tensor_tensor(w3(mxx, 2, n), w3(mxx, 2, n), nat(lses[2], 2, a, b), ALU.max), r=[mxx, lses[2]], w=[mxx])
            for g in range(3):
                eg = wk_f.get()
                V(lambda e: e.tensor_tensor(w3(eg, g, n), nat(lses[g], g, a, b), w3(mxx, g, n), ALU.subtract), r=[lses[g], mxx], w=[eg])
                A(lambda e: e.activation(eg[:, :n], eg[:, :n], AF.Exp), r=[eg], w=[eg])
                if g == 0:
                    G(lambda e: e.tensor_copy(den[:, :n], eg[:, :n]), r=[eg], w=[den])
                    V(lambda e: e.tensor_tensor(w3(num, g, n), w3(eg, g, n), nat(ogs[g], g, a, b), ALU.mult), r=[eg, ogs[g]], w=[num])
                else:
                    G(lambda e: e.tensor_tensor(den[:, :n], den[:, :n], eg[:, :n], ALU.add), r=[eg, den], w=[den])
                    V(lambda e: e.tensor_tensor(w3(eg, g, n), w3(eg, g, n), nat(ogs[g], g, a, b), ALU.mult), r=[eg, ogs[g]], w=[eg])
                    V(lambda e: e.tensor_tensor(num[:, :n], num[:, :n], eg[:, :n], ALU.add), r=[eg, num], w=[num])
            V(lambda e: e.reciprocal(den[:, :n], den[:, :n]), r=[den], w=[den])
            V(lambda e: e.tensor_tensor(num[:, :n], num[:, :n], den[:, :n], ALU.mult), r=[num, den], w=[num])
            sgt = sg_fn(a, b)
            V(lambda e: e.tensor_tensor(y_ap(a, b), num[:, :n], sgt[:, :n], ALU.mult), r=[num, sgt], w=[yres])

    def att_prompt(i, j, half):
        s1k = lambda k: s1p[:, i, k:k + 1]; s2k = lambda k: modp[:, i, k:k + 1]
        def mk_h(c, k, cs_, tmp):
            A(lambda e: e.activation(hall[:, k, cs_], tmp[:], AF.Identity, bias=s2k(k), scale=s1k(k)), r=[tmp, s1p, modp], w=[hall])
        adaln_prompt(s1k, s2k, mk_h)
        awv = wb["attin"][j].rearrange("p (h k n) -> p h k n", h=8, k=8)
        awov = wb["attout"][j].rearrange("p (h n) -> p h n", h=8)
        G(lambda e: e.memset(qT[:], 0.0), w=[qT]); G(lambda e: e.memset(qT1[:], 0.0), w=[qT1])
        for vz in (vt0, vt1, vprev0, vprev1):
            G(lambda e: e.memset(vz[:], 0.0), w=[vz])
        for hp in range(8):
            DMA(awg[:], awv[:, hp, :, 1152:1280], r=[wres["attin", j]], w=[awg])
            DMA(awo[:], awov[:, hp], r=[wres["attout", j]], w=[awo])
            for g in range(3):
                dil = DILS[g]; ncr = 16 // dil; spc = 512 // dil
                DMA(aw[:], awv[:, hp, :, g * 384:(g + 1) * 384], r=[wres["attin", j]], w=[aw])
                wq = lambda k: aw[:, k, 0:128]
                wk_ = lambda k: aw[:, k, 128:256]
                wv_ = lambda k: aw[:, k, 256:384]
                if half == 1:
                    DMA(kprev[:, 0:dil, :], carK[j, hp, g, :, 0:dil, :], r=[carK_res[j, hp, g]], w=[kprev])
                    for hh in range(2):
                        fs = slice(hh * 64, (hh + 1) * 64)
                        DMA(vprevm[hh][:, 0:dil, fs], carV[j, hp, g, :, 0:dil, fs], r=[carV_res[j, hp, g]], w=[vprevm[hh]])
                for c in range(4):
                    cs_ = slice(c * 512, (c + 1) * 512)
                    for which, wfn in ((0, wq), (1, wk_)):
                        pp = mm_pool.get()
                        for k in range(8):
                            P(lambda e: e.matmul(pp[:], lhsT=wfn(k), rhs=hall[:, k, cs_], start=(k == 0), stop=(k == 7)), r=[aw, hall], w=[pp])
                        if which == 0:
                            for hh in range(2):
                                psl = slice(hh * 64, (hh + 1) * 64)
                                dvh = qTm[hh][psl, :].rearrange("p (r s) -> p r s", r=dil)[:, :, c * spc:(c + 1) * spc]
                                svh = pp[psl, :].rearrange("p (s r) -> p r s", r=dil)
                                A(lambda e: e.mul(dvh, svh, 0.125), r=[pp], w=[qTm[hh]])
                        else:
                            dv = kT[:].rearrange("p (r s) -> p r s", r=dil)[:, :, c * spc:(c + 1) * spc]
                            sv = pp[:].rearrange("p (s r) -> p r s", r=dil)
                            A(lambda e: e.copy(dv, sv), r=[pp], w=[kT])
                for ci in range(16):
                    r_, c_ = ci // ncr, ci % ncr
                    hsl = lambda k: hall[:, k, :].rearrange("p (s r) -> p r s", r=dil)[:, r_, c_ * 128:(c_ + 1) * 128]
                    vp = mm_pool.get()
                    for k in range(8):
                        P(lambda e: e.matmul(vp[:, 0:128], lhsT=hsl(k), rhs=wv_(k), start=(k == 0), stop=(k == 7)), r=[aw, hall], w=[vp])
                    for hh in range(2):
                        fs = slice(hh * 64, (hh + 1) * 64)
                        A(lambda e: e.copy(vtm[hh][:, ci, fs], vp[:, fs]), r=[vp], w=[vtm[hh]])
                    if half == 1 and c_ == ncr - 1:
                        for k in range(8):
                            P(lambda e: e.matmul(vp[:, 128:256], lhsT=hsl(k), rhs=wk_(k), start=(k == 0), stop=(k == 7)), r=[aw, hall], w=[vp])
                        kvs = kvst_pool.get()
                        V(lambda e: e.tensor_copy(kvs[:, 0, :], vp[:, 128:256]), r=[vp], w=[kvs])
                        V(lambda e: e.tensor_copy(kvs[:, 1, :], vp[:, 0:128]), r=[vp], w=[kvs])
                        dstv = pkv_o[g][j].rearrange("(i r) kv f -> r i kv f", r=dil)[r_, :, :, hp * 128:(hp + 1) * 128]
                        DMA(dstv, kvs[:], r=[kvs], is_output=True)
                if half == 0 and 1 in halves:
                    carK_res[j, hp, g] = Res("ck"); carV_res[j, hp, g] = Res("cv")
                    DMA(carK[j, hp, g, :, 0:dil, :], kT[:].rearrange("p (r c q) -> p r c q", r=dil, c=ncr)[:, :, ncr - 1, :], r=[kT], w=[carK_res[j, hp, g]])
                    for hh in range(2):
                        fs = slice(hh * 64, (hh + 1) * 64)
                        DMA(carV[j, hp, g, :, 0:dil, fs], vtm[hh][:].rearrange("p (r c) f -> p r c f", r=dil)[:, :, ncr - 1, fs], r=[vtm[hh]], w=[carV_res[j, hp, g]])
                for ci in range(16):
                    r_, c_ = ci // ncr, ci % ncr
                    qf = lambda hh: qTm[hh][:, ci * 128:(ci + 1) * 128]
                    kts = []
                    if c_ > 0:
                        kts.append((kT[:, (ci - 1) * 128:ci * 128], [vt0[:, ci - 1, :], vt1[:, ci - 1, :]], kT, [vt0, vt1]))
                    elif half == 1:
                        kts.append((kprev[:, r_, :], [vprev0[:, r_, :], vprev1[:, r_, :]], kprev, [vprev0, vprev1]))
                    kts.append((kT[:, ci * 128:(ci + 1) * 128], [vt0[:, ci, :], vt1[:, ci, :]], kT, [vt0, vt1]))
                    odf = lambda: og[g][:, ci * 128:(ci + 1) * 128]
                    ldf = lambda: lseg[g][:, ci * 128:(ci + 1) * 128]
                    attn_chunk(128, qf, kts, g, hp, odf, ldf, og[g], lseg[g], [qT, qT1])
            def sg_fn(a_, b_):
                gp = mm_pool.get()
                for k in range(8):
                    P(lambda e: e.matmul(gp[:], lhsT=awg[:, k, :], rhs=hall[:, k, a_:b_], start=(k == 0), stop=(k == 7)), r=[awg, hall], w=[gp])
                sgt = wk_f.get()
                A(lambda e: e.activation(sgt[:], gp[:], AF.Silu), r=[gp], w=[sgt])
                return sgt
            combine(TH, og, lseg, sg_fn, lambda a, b: yhp[:, a:b], yhp)
            for c in range(4):
                cs_ = slice(c * 512, (c + 1) * 512)
                for o in range(8):
                    op_ = mm_pool.get()
                    P(lambda e: e.matmul(op_[:], lhsT=awo[:, o * 128:(o + 1) * 128], rhs=yhp[:, cs_], start=True, stop=True), r=[awo, yhp], w=[op_])
                    V(lambda e: e.scalar_tensor_tensor(x[:, o, cs_], op_[:], modp[:, i, 16 + o:17 + o], x[:, o, cs_], ALU.mult, ALU.add), r=[op_, modp, x], w=[x])

    ostage = wk_f
    for half in halves:
        for c in range(4):
            DMA(x[:, :, c * 512:(c + 1) * 512], xp[:, :, half * TH + c * 512: half * TH + (c + 1) * 512], w=[x])
        for i in range(n_layers):
            if i % 2 == 0:
                lru_prompt(i, i // 2, half)
            else:
                att_prompt(i, i // 2, half)
        def mk_out(c, k, cs_, tmp):
            ot = ostage.get()
            A(lambda e: e.activation(ot[:], tmp[:], AF.Identity, scale=finalg[:, k:k + 1]), r=[tmp, finalg], w=[ot])
            DMA(yp_o[:, k, half * TH + c * 512: half * TH + (c + 1) * 512], ot[:], r=[ot], is_output=True)
        adaln_prompt(None, None, mk_out)

    if do_sample:
        xs_t = sb("xs_t", [128, 8, NS]); h_s = sb("h_s", [128, 8, NS], BF16)
        rstd_s = sb("rstd_s", [128, NS]); vtok = sb("vtok", [128, 128], BF16)
        sconv_st = sb("sconv_st", [128, 12, NB, 3]); hst_s = sb("hst_s", [128, 12, NB])
        kvs_s = sb("kvs_s", [32, 2, 128]); kcb = sb("kcb", [128, 128], BF16)
        DMA(xs_t[:], xs, w=[xs_t])
        G(lambda e: e.memset(vtok[:], 0.0), w=[vtok])
        for t_ in mexp_pool.tiles + pb_pool.tiles:
            G(lambda e: e.memset(t_[:], 0.0), w=[t_])

        def adaln_sample(i, final=False):
            ssq = mm_pool.get()
            for k in range(8):
                sq = wk_f.get()
                A(lambda e: e.activation(sq[:, 0:NS], xs_t[:, k, :], AF.Square), r=[xs_t], w=[sq])
                P(lambda e: e.matmul(ssq[:, 0:NS], lhsT=onesdiv[:], rhs=sq[:, 0:NS], start=(k == 0), stop=(k == 7)), r=[onesdiv, sq], w=[ssq])
            A(lambda e: e.activation(rstd_s[:], ssq[:, 0:NS], AF.Sqrt, bias=EPS), r=[ssq], w=[rstd_s])
            V(lambda e: e.reciprocal(rstd_s[:], rstd_s[:]), r=[rstd_s], w=[rstd_s])
            for k in range(8):
                tmp = wk_f.get()
                V(lambda e: e.tensor_tensor(tmp[:, 0:NS], xs_t[:, k, :], rstd_s[:], ALU.mult), r=[xs_t, rstd_s], w=[tmp])
                if final:
                    A(lambda e: e.activation(tmp[:, 0:NS], tmp[:, 0:NS], AF.Identity, scale=finalg[:, k:k + 1]), r=[tmp, finalg], w=[tmp])
                    DMA(ys_o[:, k, :], tmp[:, 0:NS], r=[tmp], is_output=True)
                else:
                    for b_ in range(NB):
                        bs_ = slice(b_ * 8, (b_ + 1) * 8)
                        V(lambda e: e.tensor_scalar(h_s[:, k, bs_], tmp[:, bs_], s1s[:, i, k, b_:b_ + 1], mods[:, i, k, b_:b_ + 1], ALU.mult, ALU.add), r=[tmp, s1s, mods], w=[h_s])

        def resid_sample(i, o, op_):
            for b_ in range(NB):
                bs_ = slice(b_ * 8, (b_ + 1) * 8)
                V(lambda e: e.scalar_tensor_tensor(xs_t[:, o, bs_], op_[:, bs_], mods[:, i, 16 + o, b_:b_ + 1], xs_t[:, o, bs_], ALU.mult, ALU.add), r=[op_, mods, xs_t], w=[xs_t])

        def lru_sample(i, j):
            adaln_sample(i)
            DMA(sconv_st[:], sconv_in[j], w=[sconv_st]); DMA(hst_s[:], sh_in[j], w=[hst_s])
            win = wb["lruin"][j].rearrange("p (g t k n) -> p g t k n", g=4, t=2, k=8)
            wov = wb["lruout"][j].rearrange("p (g m n) -> p g m n", g=4, m=3)
            wav = wb["wa"][j].rearrange("p (g m n) -> p g m n", g=4, m=3)
            wxv = wb["wx"][j].rearrange("p (g m n) -> p g m n", g=4, m=3)
            for g4 in range(4):
                wxb = lw_pool.get(); wgb = lw_pool.get()
                DMA(wxb[:], win[:, g4, 0], r=[wres["lruin", j]], w=[wxb])
                DMA(wgb[:], win[:, g4, 1], r=[wres["lruin", j]], w=[wgb])
                DMA(lwo[:], wov[:, g4], r=[wres["lruout", j]], w=[lwo])
                DMA(lwa[:], wav[:, g4], r=[wres["wa", j]], w=[lwa])
                DMA(lwx[:], wxv[:, g4], r=[wres["wx", j]], w=[lwx])
                for m in range(3):
                    ct = g4 * 3 + m
                    pp = mm_pool.get()
                    for k in range(8):
                        P(lambda e: e.matmul(pp[:, 0:NS], lhsT=wxb[:, k, m * 128:(m + 1) * 128], rhs=h_s[:, k, :], start=(k == 0), stop=(k == 7)), r=[wxb, h_s], w=[pp])
                    xpad = xpad_t[m]; xp3 = xpad[:, 0:44].rearrange("p (b t) -> p b t", b=NB)
                    G(lambda e: e.tensor_copy(xp3[:, :, 0:3], sconv_st[:, ct, :, :]), r=[sconv_st], w=[xpad])
                    A(lambda e: e.copy(xp3[:, :, 3:11], pp[:, 0:NS].rearrange("p (b t) -> p b t", b=NB)), r=[pp], w=[xpad])
                    G(lambda e: e.tensor_copy(sconv_st[:, ct, :, :], xp3[:, :, 8:11]), r=[xpad], w=[sconv_st])
                    xc = xc_t[m]; xc3 = xc[:, 0:NS].rearrange("p (b t) -> p b t", b=NB)
                    V(lambda e: e.tensor_scalar(xc3, xp3[:, :, 0:8], convw[:, j, 0, ct:ct + 1], convb[:, j, ct:ct + 1], ALU.mult, ALU.add), r=[xpad, convw, convb], w=[xc])
                    for tp in range(1, 4):
                        V(lambda e: e.scalar_tensor_tensor(xc3, xp3[:, :, tp:tp + 8], convw[:, j, tp, ct:ct + 1], xc3, ALU.mult, ALU.add), r=[xpad, convw, xc], w=[xc])
                    A(lambda e: e.copy(xcb_t[m][:, 0:NS], xc[:, 0:NS]), r=[xc], w=[xcb_t[m]])
                for m in range(3):
                    ct = g4 * 3 + m
                    kks = [kk for kk in range(3) if not ((m == 0 and kk == 2) or (m == 2 and kk == 0))]
                    rp = mm_pool.get()
                    for n_, kk in enumerate(kks):
                        P(lambda e: e.matmul(rp[:, 0:NS], lhsT=lwa[:, kk, m * 128:(m + 1) * 128], rhs=xcb_t[kk][:, 0:NS], start=(n_ == 0), stop=(n_ == len(kks) - 1)), r=[lwa, xcb_t[kk]], w=[rp])
                    rr = wk_f.get()
                    A(lambda e: e.activation(rr[:, 0:NS], rp[:, 0:NS], AF.Sigmoid, bias=ba[:, j, ct:ct + 1]), r=[rp, ba], w=[rr])
                    ip = mm_pool.get()
                    for n_, kk in enumerate(kks):
                        P(lambda e: e.matmul(ip[:, 0:NS], lhsT=lwx[:, kk, m * 128:(m + 1) * 128], rhs=xcb_t[kk][:, 0:NS], start=(n_ == 0), stop=(n_ == len(kks) - 1)), r=[lwx, xcb_t[kk]], w=[ip])
                    ig = wk_f.get()
                    A(lambda e: e.activation(ig[:, 0:NS], ip[:, 0:NS], AF.Sigmoid, bias=bx[:, j, ct:ct + 1]), r=[ip, bx], w=[ig])
                    aa = wk_f.get()
                    A(lambda e: e.activation(aa[:, 0:NS], rr[:, 0:NS], AF.Exp, scale=cA[:, j, ct:ct + 1]), r=[rr, cA], w=[aa])
                    A(lambda e: e.activation(rr[:, 0:NS], rr[:, 0:NS], AF.Exp, scale=cA2[:, j, ct:ct + 1]), r=[rr, cA2], w=[rr])
                    V(lambda e: e.tensor_scalar(rr[:, 0:NS], rr[:, 0:NS], -1.0, 1.0, ALU.mult, ALU.add), r=[rr], w=[rr])
                    V(lambda e: e.tensor_scalar(rr[:, 0:NS], rr[:, 0:NS], 0.0, None, ALU.max), r=[rr], w=[rr])
                    A(lambda e: e.activation(rr[:, 0:NS], rr[:, 0:NS], AF.Sqrt), r=[rr], w=[rr])
                    V(lambda e: e.tensor_tensor(ig[:, 0:NS], ig[:, 0:NS], rr[:, 0:NS], ALU.mult), r=[ig, rr], w=[ig])
                    V(lambda e: e.tensor_tensor(ig[:, 0:NS], ig[:, 0:NS], xc_t[m][:, 0:NS], ALU.mult), r=[ig, xc_t[m]], w=[ig])
                    hs = rr
                    for b_ in range(NB):
                        bs_ = slice(b_ * 8, (b_ + 1) * 8)
                        V(lambda e: e.tensor_tensor_scan(hs[:, bs_], aa[:, bs_], ig[:, bs_], hst_s[:, ct, b_:b_ + 1], ALU.mult, ALU.add), r=[aa, ig, hst_s], w=[hs])
                        G(lambda e: e.tensor_copy(hst_s[:, ct, b_:b_ + 1], hs[:, b_ * 8 + 7:b_ * 8 + 8]), r=[hs], w=[hst_s])
                    gp = mm_pool.get()
                    for k in range(8):
                        P(lambda e: e.matmul(gp[:, 0:NS], lhsT=wgb[:, k, m * 128:(m + 1) * 128], rhs=h_s[:, k, :], start=(k == 0), stop=(k == 7)), r=[wgb, h_s], w=[gp])
                    sg = aa
                    A(lambda e: e.activation(sg[:, 0:NS], gp[:, 0:NS], AF.Silu), r=[gp], w=[sg])
                    V(lambda e: e.tensor_tensor(ybuf[m][:, 0:NS], hs[:, 0:NS], sg[:, 0:NS], ALU.mult), r=[hs, sg], w=[ybuf[m]])
                for o in range(8):
                    op_ = mm_pool.get()
                    for m in range(3):
                        P(lambda e: e.matmul(op_[:, 0:NS], lhsT=lwo[:, m, o * 128:(o + 1) * 128], rhs=ybuf[m][:, 0:NS], start=(m == 0), stop=(m == 2)), r=[lwo, ybuf[m]], w=[op_])
                    resid_sample(i, o, op_)
            DMA(sconv_o[j], sconv_st[:], r=[sconv_st], is_output=True)
            DMA(sh_o[j], hst_s[:], r=[hst_s], is_output=True)

        def att_sample(i, j):
            adaln_sample(i)
            awv = wb["attin"][j].rearrange("p (h k n) -> p h k n", h=8, k=8)
            awov = wb["attout"][j].rearrange("p (h n) -> p h n", h=8)
            G(lambda e: e.memset(qT[:], 0.0), w=[qT]); G(lambda e: e.memset(qT1[:], 0.0), w=[qT1])
            G(lambda e: e.memset(kT[:], 0.0), w=[kT])
            for vz in (vt0, vt1):
                G(lambda e: e.memset(vz[:], 0.0), w=[vz])
            ks = kT[:, 0:NS]; kTc = kT[:, 128:256]; kTn = kT[:, 256:384]
            G(lambda e: e.memset(hall[:, :, 0:128], 0.0), w=[hall])
            A(lambda e: e.copy(hall[:, :, 0:NS], h_s[:]), r=[h_s], w=[hall])
            vc = [vt0[:, 0, :], vt1[:, 0, :]]; vn = [vt0[:, 1, :], vt1[:, 1, :]]
            for hp in range(8):
                DMA(awg[:], awv[:, hp, :, 1152:1280], r=[wres["attin", j]], w=[awg])
                DMA(awo[:], awov[:, hp], r=[wres["attout", j]], w=[awo])
                for g in range(3):
                    dil = DILS[g]
                    DMA(aw[:], awv[:, hp, :, g * 384:(g + 1) * 384], r=[wres["attin", j]], w=[aw])
                    if SSU < 1: continue
                    pp = mm_pool.get()
                    for k in range(8):
                        P(lambda e: e.matmul(pp[:, 0:NS], lhsT=aw[:, k, 0:128], rhs=h_s[:, k, :], start=(k == 0), stop=(k == 7)), r=[aw, h_s], w=[pp])
                    for hh in range(2):
                        psl = slice(hh * 64, (hh + 1) * 64)
                        A(lambda e: e.mul(qTm[hh][psl, 0:NS], pp[psl, 0:NS], 0.125), r=[pp], w=[qTm[hh]])
                    pk = mm_pool.get()
                    for k in range(8):
                        P(lambda e: e.matmul(pk[:, 0:NS], lhsT=aw[:, k, 128:256], rhs=h_s[:, k, :], start=(k == 0), stop=(k == 7)), r=[aw, h_s], w=[pk])
                    A(lambda e: e.copy(ks, pk[:, 0:NS]), r=[pk], w=[kT])
                    if SSU < 2: continue
                    kvp = mm_pool.get()
                    for t_ in range(2):
                        for k in range(8):
                            P(lambda e: e.matmul(kvp[:, t_ * 128:(t_ + 1) * 128], lhsT=hall[:, k, 0:128], rhs=aw[:, k, (1 + t_) * 128:(2 + t_) * 128], start=(k == 0), stop=(k == 7)), r=[aw, hall], w=[kvp])
                    if SSU < 2.5: continue
                    kvs = wk_f.get()
                    A(lambda e: e.copy(kvs[:, 0:256], kvp[:, 0:256]), r=[kvp], w=[kvs])
                    A(lambda e: e.copy(vtok[:], kvp[:, 128:256]), r=[kvp], w=[vtok])
                    if SSU < 3: continue
                    DMA(skv_o[g][j].rearrange("b t kv f -> (b t) kv f")[:, :, hp * 128:(hp + 1) * 128], kvs[0:NS, 0:256].rearrange("p (a b) -> p a b", a=2), r=[kvs], is_output=True)
                    nres = min(dil, 8); nq = 8 // nres
                    SST = 9.0
                    if SSU < 4: continue
                    for b_ in range(NB):
                        for r_ in range(nres):
                            c0 = b_ * 8 + r_
                            csel = slice(c0, c0 + dil * (nq - 1) + 1, dil)
                            csel8 = slice(c0, c0 + dil * 7 + 1, dil)
                            kvc = wk_f.get()
                            kvc3 = kvc[:, 0:256].rearrange("p (a b) -> p a b", a=2)
                            src = caches[g][j, b_].rearrange("(s r) (kv f) -> r s kv f", r=dil, kv=2)[r_, :, :, hp * 128:(hp + 1) * 128]
                            DMA(kvc3, src, w=[kvc])
                            A(lambda e: e.copy(kcb[:], kvc[:, 0:128]), r=[kvc], w=[kcb])
                            P(lambda e: e.transpose(pt_ps[:, 0:128], kcb[:], identb[:, 0:128]), r=[kcb, identb], w=[pt_ps])
                            A(lambda e: e.copy(kTc, pt_ps[:, 0:128]), r=[pt_ps], w=[kT])
                            for hh in range(2):
                                fs = slice(hh * 64, (hh + 1) * 64)
                                G(lambda e: e.tensor_copy(vc[hh][:, fs], kvc[:, 128 + hh * 64:128 + (hh + 1) * 64]), r=[kvc], w=[vtm[hh]])
                            A(lambda e: e.copy(kTn[:, 0:8], kT[:, csel8]), r=[kT], w=[kT])
                            vnp = mm_pool.get()
                            P(lambda e: e.matmul(vnp[0:8, 0:128], lhsT=identb[:, csel8], rhs=vtok[:], start=True, stop=True), r=[identb, vtok], w=[vnp])
                            for hh in range(2):
                                fs = slice(hh * 64, (hh + 1) * 64)
                                A(lambda e: e.copy(vn[hh][0:8, fs], vnp[0:8, fs]), r=[vnp], w=[vtm[hh]])
                            if SST < 3: continue
                            if SST < 4 and g > 0: continue
                            if SST < 5 and g > 1: continue
                            qf = lambda hh: qTm[hh][:, csel8]
                            kts = [(kTc, vc, kT, [vt0, vt1]), (kTn, vn, kT, [vt0, vt1])]
                            odf = lambda: og[g][:, csel]
                            ldf = lambda: lseg[g][:, csel]
                            attn_chunk(8, qf, kts, g, hp, odf, ldf, og[g], lseg[g], [qT, qT1], nw=nq)
                def sg_fn(a_, b_2):
                    gp = mm_pool.get()
                    for k in range(8):
                        P(lambda e: e.matmul(gp[:, 0:NS], lhsT=awg[:, k, :], rhs=h_s[:, k, :], start=(k == 0), stop=(k == 7)), r=[awg, h_s], w=[gp])
                    sgt = wk_f.get()
                    A(lambda e: e.activation(sgt[:, 0:NS], gp[:, 0:NS], AF.Silu), r=[gp], w=[sgt])
                    return sgt
                combine(NS, og, lseg, sg_fn, lambda a, b: yhp[:, a:b], yhp, dils=(1, 1, 1))
                for o in range(8):
                    op_ = mm_pool.get()
                    P(lambda e: e.matmul(op_[:, 0:NS], lhsT=awo[:, o * 128:(o + 1) * 128], rhs=yhp[:, 0:NS], start=True, stop=True), r=[awo, yhp], w=[op_])
                    resid_sample(i, o, op_)

        SSU = 9.0
        for i in range(n_layers):
            if i % 2 == 0:
                lru_sample(i, i // 2)
            elif SSU >= 0:
                att_sample(i, i // 2)
        adaln_sample(0, final=True)
    fw.finish()
    st.close()
    return nc


def sample_pass(L):
    pass


def _fm(v, k):
    return np.ascontiguousarray(v.reshape(k, 128).T)


def _prep_shared(inp):
    f32 = np.float32
    sh = {}
    sh["normg"] = np.ascontiguousarray(inp["norm_g"].reshape(4, 8, 128).transpose(2, 0, 1)).astype(f32)
    sh["adab"] = np.ascontiguousarray(inp["ada_b"].reshape(4, 24, 128).transpose(2, 0, 1))
    sh["finalg"] = _fm(inp["final_g"], 8)
    sh["convw"] = np.ascontiguousarray(inp["lru_conv_w"].reshape(2, 4, 12, 128).transpose(3, 0, 1, 2))
    for n, k in (("convb", "lru_conv_b"), ("ba", "lru_ba"), ("bx", "lru_bx"), ("lam", "lru_lambda")):
        sh[n] = np.ascontiguousarray(inp[k].reshape(2, 12, 128).transpose(2, 0, 1))
    sh["w_adaw"] = np.ascontiguousarray(inp["ada_w"].reshape(4, 8, 128, 3072).transpose(0, 2, 1, 3)).reshape(4, 128, 24576)
    w = inp["lru_w_in"].reshape(2, 8, 128, 2, 4, 384)
    sh["w_lruin"] = np.ascontiguousarray(w.transpose(0, 2, 4, 3, 1, 5)).reshape(2, 128, 24576)
    w = inp["lru_w_out"].reshape(2, 4, 3, 128, 1024)
    sh["w_lruout"] = np.ascontiguousarray(w.transpose(0, 3, 1, 2, 4)).reshape(2, 128, 12288)
    for n, k in (("w_wa", "lru_wa"), ("w_wx", "lru_wx")):
        src = inp[k]
        dense = np.zeros((2, 4, 384, 384), np.float32)
        for blk in range(16):
            g, b = blk // 4, blk % 4
            dense[:, g, b * 96:(b + 1) * 96, b * 96:(b + 1) * 96] = src[:, blk]
        d = dense.reshape(2, 4, 3, 128, 384)
        sh[n] = np.ascontiguousarray(d.transpose(0, 3, 1, 2, 4)).reshape(2, 128, 4608)
    w = inp["att_w_in"]
    qkv = w[:, :, :9216].reshape(2, 8, 128, 3, 3, 8, 128)
    gate = w[:, :, 9216:].reshape(2, 8, 128, 8, 128)
    blk = np.empty((2, 128, 8, 8, 10, 128), np.float32)
    blk[:, :, :, :, :9, :] = qkv.transpose(0, 2, 5, 1, 3, 4, 6).reshape(2, 128, 8, 8, 9, 128)
    blk[:, :, :, :, 9, :] = gate.transpose(0, 2, 3, 1, 4)
    sh["w_attin"] = blk.reshape(2, 128, 81920)
    w = inp["att_w_out"].reshape(2, 8, 128, 1024)
    sh["w_attout"] = np.ascontiguousarray(w.transpose(0, 2, 1, 3)).reshape(2, 128, 8192)
    sh["identf"] = np.eye(128, dtype=f32)
    sh["identb"] = np.concatenate([np.eye(128), np.zeros((128, 128))], axis=1).astype(ml_dtypes.bfloat16)
    sh["onesdiv"] = np.full((128, 128), 1.0 / 1024, f32)
    om = np.zeros((128, 2, 128), np.float32); om[:, 0, 0:64] = 1.0; om[:, 1, 64:128] = 1.0
    sh["onesb"] = om.astype(ml_dtypes.bfloat16)
    ii = np.arange(128)[:, None]; jj = np.arange(256)[None, :]
    diff = ii - jj + 128
    sh["dmat"] = np.where((diff >= 0) & (diff <= 128), diff, BIGD).astype(f32)
    return sh


_NC_CACHE = {}


def kernel(**inp):
    inp = {k: np.asarray(v) for k, v in inp.items()}
    sh = _prep_shared(inp)
    key = "full"
    if key not in _NC_CACHE:
        _NC_CACHE[key] = build()
    nc = _NC_CACHE[key]
    in_maps = []
    for c in range(8):
        s = c % 4
        m = dict(sh)
        m["xp"] = np.ascontiguousarray(inp["x_prompt"][s].T.reshape(8, 128, SEQ).transpose(1, 0, 2))
        m["cp"] = _fm(inp["c_prompt"][s], 8)
        bs = slice(c * NB, (c + 1) * NB)
        m["xs"] = np.ascontiguousarray(inp["x_sample"][bs].reshape(NS, 8, 128).transpose(2, 1, 0))
        m["cs"] = np.ascontiguousarray(inp["c_sample"][bs].reshape(NB, 8, 128).transpose(2, 1, 0))
        m["sconv_in"] = np.ascontiguousarray(inp["state_conv"][:, bs].reshape(2, NB, 3, 12, 128).transpose(0, 4, 3, 1, 2))
        m["sh_in"] = np.ascontiguousarray(inp["state_h"][:, bs].reshape(2, NB, 12, 128).transpose(0, 3, 2, 1))
        for g in range(3):
            m[f"cache{g}"] = np.ascontiguousarray(inp[f"cache_kv_g{g}"][:, bs].reshape(2, NB, WINS[g], 2048))
        in_maps.append(m)
    res = run_bass_kernel_spmd(nc, in_maps, core_ids=list(range(8))).results
    f32 = np.float32
    yp = np.stack([res[s]["yp_o"].transpose(2, 1, 0).reshape(SEQ, DM) for s in range(4)]).astype(f32)
    ys = np.concatenate([res[c]["ys_o"].transpose(2, 1, 0).reshape(NB, 8, DM) for c in range(8)]).astype(f32)
    pconv = np.stack([res[s]["pconv_o"].transpose(0, 3, 2, 1).reshape(2, 3, DR) for s in range(4)], axis=1).astype(f32)
    ph = np.stack([res[s]["ph_o"].transpose(0, 2, 1).reshape(2, DR) for s in range(4)], axis=1).astype(f32)
    pkv = [np.stack([res[s][f"pkv{g}_o"].reshape(2, WINS[g], 2, NHD, HD) for s in range(4)], axis=1).astype(f32) for g in range(3)]
    sconv = np.concatenate([res[c]["sconv_o"].transpose(0, 3, 4, 2, 1).reshape(2, NB, 3, DR) for c in range(8)], axis=1).astype(f32)
    shh = np.concatenate([res[c]["sh_o"].transpose(0, 3, 2, 1).reshape(2, NB, DR) for c in range(8)], axis=1).astype(f32)
    skv = [np.concatenate([res[c][f"skv{g}_o"].reshape(2, NB, 8, 2, NHD, HD) for c in range(8)], axis=1).astype(f32) for g in range(3)]
    return (yp, ys, pconv, ph, pkv[0], pkv[1], pkv[2], sconv, shh, skv[0], skv[1], skv[2])
```
